# Optimizing a Trainium2 kernel written in Bass

```python
import jax, jax.numpy as jnp
from jax import lax
import numpy as np

D_MODEL = 1024
BATCH = 8
SEQ = 2048
DEPTH = 2

PLE_DIM = 256
D_A = D_MODEL // 2
NH_A = 8
G_A = 2
HPG_A = NH_A // G_A
DK_A = D_A // NH_A
KV_A = G_A * DK_A
L_CMP = 32
STRIDE_CMP = 16
CMP_HID = 128
L_SEL = 64
N_SEL = 8
WINDOW = 256
Q_BLOCK = 128
BIG = 1e9
D_B = D_MODEL // 2
CHUNK_B = 128
G_B = 4
CH_B = D_B // G_B
D_C = D_MODEL // 2
NH_C = 4
DH_C = D_C // NH_C
CONV_W = 4
CHUNK_C = 128
N_BRANCH = 3
ALPHA = (2.0 * DEPTH) ** 0.25
BETA = (8.0 * DEPTH) ** -0.25
LN_EPS = 1e-5
IN_SIZES = (D_A, KV_A, KV_A, KV_A, KV_A, KV_A, KV_A, 3 * NH_A, D_A,
            D_B, D_B, D_B,
            D_C, D_C, D_C, 2 * NH_C, D_C, D_C,
            N_BRANCH * D_MODEL)
N_IN = sum(IN_SIZES)

kernel_name = 'hybrid_nsa_gmlp_mlstm'


def _layernorm(x, g, b):
    xf = x.astype(jnp.float32)
    mu = jnp.mean(xf, -1, keepdims=True)
    var = jnp.mean(jnp.square(xf - mu), -1, keepdims=True)
    return ((xf - mu) * lax.rsqrt(var + LN_EPS) * g + b).astype(x.dtype)


def _masked_softmax(s, mask):
    s = jnp.where(mask, s, -jnp.inf)
    m = jnp.max(s, axis=-1, keepdims=True)
    m = jnp.where(jnp.isfinite(m), m, 0.0)
    e = jnp.where(mask, jnp.exp(s - m), 0.0)
    return e / jnp.maximum(jnp.sum(e, -1, keepdims=True), 1e-30)


def _alibi_slopes(n):
    return 2.0 ** (-8.0 * jnp.arange(1, n + 1, dtype=jnp.float32) / n)


def _nsa(q, kc, vc, ks, vs, kw, vw, g, pos_k, pos_v, wk1, wk2, wv1, wv2):
    bsz, t_len, _ = q.shape
    f32 = jnp.float32
    qh = (q * DK_A ** -0.5).reshape(bsz, t_len, G_A, HPG_A, DK_A).transpose(0, 2, 3, 1, 4)
    slopes = _alibi_slopes(NH_A).reshape(G_A, HPG_A, 1, 1)
    tpos = jnp.arange(t_len)

    def kv_heads(z):
        return z.reshape(bsz, t_len, G_A, DK_A)

    n_cmp = (t_len - L_CMP) // STRIDE_CMP + 1
    cmp_start = jnp.arange(n_cmp) * STRIDE_CMP
    blk = cmp_start[:, None] + jnp.arange(L_CMP)[None, :]

    def compress(z, pos, w1, w2):
        zb = kv_heads(z)[:, blk] + pos[:, None, :]
        zb = zb.transpose(0, 1, 3, 2, 4).reshape(bsz, n_cmp, G_A, L_CMP * DK_A)
        return (jax.nn.gelu(zb @ w1) @ w2).transpose(0, 2, 1, 3)

    k_cmp = compress(kc, pos_k, wk1, wk2)
    v_cmp = compress(vc, pos_v, wv1, wv2)
    d_cmp = tpos[:, None] - (cmp_start + L_CMP - 1)[None, :]
    s_cmp = jnp.einsum('bghtd,bgnd->bghtn', qh, k_cmp).astype(f32) - slopes * d_cmp
    p_cmp = _masked_softmax(s_cmp, d_cmp >= 0)
    o_cmp = jnp.einsum('bghtn,bgnd->bghtd', p_cmp.astype(v_cmp.dtype), v_cmp)

    n_blk = t_len // L_SEL
    sel_start = jnp.arange(n_blk) * L_SEL
    overlap = ((cmp_start[:, None] < sel_start[None, :] + L_SEL)
               & (cmp_start[:, None] + L_CMP > sel_start[None, :])).astype(f32)
    imp = jnp.einsum('bghtn,nj->bgtj', p_cmp, overlap)
    cur = (tpos // L_SEL)[:, None]
    jb = jnp.arange(n_blk)[None, :]
    forced = (jb == 0) | (jb == cur) | (jb == cur - 1)
    score = jnp.where(jb <= cur, jnp.where(forced, BIG, imp), -BIG)
    k_top = min(N_SEL, n_blk)
    top_score, top_idx = lax.top_k(score, k_top)
    top_valid = top_score > -0.5 * BIG
    n_tok = k_top * L_SEL

    ks_h = kv_heads(ks).transpose(0, 2, 1, 3)
    vs_h = kv_heads(vs).transpose(0, 2, 1, 3)
    pad = ((0, 0), (0, 0), (WINDOW, 0), (0, 0))
    kw_h = jnp.pad(kv_heads(kw).transpose(0, 2, 1, 3), pad)
    vw_h = jnp.pad(kv_heads(vw).transpose(0, 2, 1, 3), pad)

    def query_block(qb):
        t0 = qb * Q_BLOCK
        tq = t0 + jnp.arange(Q_BLOCK)
        q_blk = lax.dynamic_slice_in_dim(qh, t0, Q_BLOCK, axis=3)
        idx = lax.dynamic_slice_in_dim(top_idx, t0, Q_BLOCK, axis=2)
        val = lax.dynamic_slice_in_dim(top_valid, t0, Q_BLOCK, axis=2)
        tok = (idx[..., None] * L_SEL + jnp.arange(L_SEL)).reshape(bsz, G_A, Q_BLOCK, n_tok)
        m_sel = jnp.repeat(val, L_SEL, axis=-1) & (tok <= tq[:, None])
        flat = tok.reshape(bsz, G_A, Q_BLOCK * n_tok, 1)
        k_g = jnp.take_along_axis(ks_h, flat, axis=2).reshape(bsz, G_A, Q_BLOCK, n_tok, DK_A)
        v_g = jnp.take_along_axis(vs_h, flat, axis=2).reshape(bsz, G_A, Q_BLOCK, n_tok, DK_A)
        d_sel = (tq[:, None] - tok)[:, :, None]
        s_sel = jnp.einsum('bghqd,bgqsd->bghqs', q_blk, k_g).astype(f32) - slopes * d_sel
        p_sel = _masked_softmax(s_sel, m_sel[:, :, None])
        o_sel = jnp.einsum('bghqs,bgqsd->bghqd', p_sel.astype(v_g.dtype), v_g)
        k_w = lax.dynamic_slice_in_dim(kw_h, t0, WINDOW + Q_BLOCK, axis=2)
        v_w = lax.dynamic_slice_in_dim(vw_h, t0, WINDOW + Q_BLOCK, axis=2)
        sk = t0 - WINDOW + jnp.arange(WINDOW + Q_BLOCK)
        d_w = tq[:, None] - sk[None, :]
        m_w = (d_w >= 0) & (d_w < WINDOW) & (sk[None, :] >= 0)
        s_w = jnp.einsum('bghqd,bgkd->bghqk', q_blk, k_w).astype(f32) - slopes * d_w
        p_w = _masked_softmax(s_w, m_w)
        o_w = jnp.einsum('bghqk,bgkd->bghqd', p_w.astype(v_w.dtype), v_w)
        return o_sel, o_w

    o_sel, o_win = lax.map(query_block, jnp.arange(t_len // Q_BLOCK))
    o_sel = o_sel.transpose(1, 2, 3, 0, 4, 5).reshape(bsz, G_A, HPG_A, t_len, DK_A)
    o_win = o_win.transpose(1, 2, 3, 0, 4, 5).reshape(bsz, G_A, HPG_A, t_len, DK_A)
    gates = jax.nn.sigmoid(g).reshape(bsz, t_len, 3, G_A, HPG_A).transpose(2, 0, 3, 4, 1)[..., None]
    o = gates[0] * o_cmp + gates[1] * o_sel + gates[2] * o_win
    return o.transpose(0, 3, 1, 2, 4).reshape(bsz, t_len, D_A)


def _spatial_gating(u, v, ln_g, ln_b, w_s, b_s):
    bsz, t_len, _ = u.shape
    u = jax.nn.gelu(u)
    v = _layernorm(jax.nn.gelu(v), ln_g, ln_b)
    v = v.reshape(bsz, t_len // CHUNK_B, CHUNK_B, G_B, CH_B)
    w = w_s * jnp.tril(jnp.ones((CHUNK_B, CHUNK_B), w_s.dtype))
    mixed = jnp.einsum('gts,bcsge->bctge', w, v) + b_s.T[:, :, None]
    return u * mixed.reshape(bsz, t_len, D_B)


def _mlstm(q, k, v, gif, o, conv_w, conv_b, b_i, b_f, norm_g):
    bsz, t_len, _ = q.shape
    dt = q.dtype
    f32 = jnp.float32
    qk = jnp.concatenate([q, k], -1)
    qk = lax.conv_general_dilated(qk, conv_w[:, None, :], window_strides=(1,), padding=[(CONV_W - 1, 0)],
                                  dimension_numbers=('NWC', 'WIO', 'NWC'), feature_group_count=2 * D_C) + conv_b
    q, k = jnp.split(jax.nn.silu(qk), 2, axis=-1)
    nc = t_len // CHUNK_C

    def hc(z):
        return z.reshape(bsz, nc, CHUNK_C, NH_C, DH_C).transpose(0, 3, 1, 2, 4).astype(f32)

    def hg(z):
        return z.reshape(bsz, nc, CHUNK_C, NH_C).transpose(0, 3, 1, 2)

    qh, kh, vh = hc(q), hc(k) * DH_C ** -0.5, hc(v)
    gi, gf = jnp.split(gif.astype(f32), 2, axis=-1)
    log_i = hg(gi + b_i)
    log_f = jax.nn.log_sigmoid(hg(gf + b_f))
    bcum = jnp.cumsum(log_f, axis=-1)
    b_last = bcum[..., -1]
    causal = jnp.tril(jnp.ones((CHUNK_C, CHUNK_C), bool))
    dmat = jnp.where(causal, bcum[..., :, None] - bcum[..., None, :] + log_i[..., None, :], -jnp.inf)
    w_end = b_last[..., None] - bcum + log_i
    m_loc = jnp.max(w_end, -1)
    e_end = jnp.exp(w_end - m_loc[..., None])
    c_loc = jnp.einsum('bhcs,bhcsv,bhcsk->bhcvk', e_end, vh, kh)
    n_loc = jnp.einsum('bhcs,bhcsk->bhck', e_end, kh)

    def step(carry, xs):
        c_st, n_st, m_st = carry
        cl, nl, ml, bl = xs
        m_new = jnp.maximum(bl + m_st, ml)
        a = jnp.exp(bl + m_st - m_new)
        bb = jnp.exp(ml - m_new)
        c_new = a[..., None, None] * c_st + bb[..., None, None] * cl
        n_new = a[..., None] * n_st + bb[..., None] * nl
        return (c_new, n_new, m_new), (c_st, n_st, m_st)

    init = (jnp.zeros((bsz, NH_C, DH_C, DH_C), f32), jnp.zeros((bsz, NH_C, DH_C), f32), jnp.zeros((bsz, NH_C), f32))
    xs = (c_loc.transpose(2, 0, 1, 3, 4), n_loc.transpose(2, 0, 1, 3), m_loc.transpose(2, 0, 1), b_last.transpose(2, 0, 1))
    _, (c_prev, n_prev, m_prev) = lax.scan(step, init, xs)
    c_prev = c_prev.transpose(1, 2, 0, 3, 4)
    n_prev = n_prev.transpose(1, 2, 0, 3)
    m_prev = m_prev.transpose(1, 2, 0)
    inter = bcum + m_prev[..., None]
    m_t = jnp.maximum(inter, jnp.max(dmat, -1))
    s = jnp.einsum('bhctd,bhcsd->bhcts', qh, kh) * jnp.exp(dmat - m_t[..., None])
    e_int = jnp.exp(inter - m_t)
    num = jnp.einsum('bhcts,bhcsv->bhctv', s, vh) + e_int[..., None] * jnp.einsum('bhcvk,bhctk->bhctv', c_prev, qh)
    den = jnp.sum(s, -1) + e_int * jnp.einsum('bhck,bhctk->bhct', n_prev, qh)
    h = num / jnp.maximum(jnp.abs(den), jnp.exp(-m_t))[..., None]
    mu = jnp.mean(h, -1, keepdims=True)
    var = jnp.mean(jnp.square(h - mu), -1, keepdims=True)
    hn = ((h - mu) * lax.rsqrt(var + LN_EPS)).transpose(0, 2, 3, 1, 4).reshape(bsz, t_len, D_C) * norm_g
    return (hn * jax.nn.sigmoid(o.astype(f32))).astype(dt)


def setup_inputs(seed: int = 0) -> dict:
    key = jax.random.key(seed)
    ks = jax.random.split(key, 32)
    f32 = jnp.float32

    def nrm(k, shape, scale):
        return jax.random.normal(k, shape, f32) * scale

    return {
        'x': nrm(ks[0], (BATCH, SEQ, D_MODEL), 1.0),
        'p': nrm(ks[1], (DEPTH, BATCH, SEQ, PLE_DIM), 1.0),
        'w_in': nrm(ks[2], (DEPTH, D_MODEL, N_IN), D_MODEL ** -0.5),
        'cmp_pos_k': nrm(ks[3], (DEPTH, L_CMP, DK_A), 0.1),
        'cmp_pos_v': nrm(ks[4], (DEPTH, L_CMP, DK_A), 0.1),
        'cmp_wk1': nrm(ks[5], (DEPTH, L_CMP * DK_A, CMP_HID), (L_CMP * DK_A) ** -0.5),
        'cmp_wk2': nrm(ks[6], (DEPTH, CMP_HID, DK_A), CMP_HID ** -0.5),
        'cmp_wv1': nrm(ks[7], (DEPTH, L_CMP * DK_A, CMP_HID), (L_CMP * DK_A) ** -0.5),
        'cmp_wv2': nrm(ks[8], (DEPTH, CMP_HID, DK_A), CMP_HID ** -0.5),
        'sg_ln_g': 1.0 + nrm(ks[9], (DEPTH, D_B), 0.02),
        'sg_ln_b': nrm(ks[10], (DEPTH, D_B), 0.02),
        'sg_w': nrm(ks[11], (DEPTH, G_B, CHUNK_B, CHUNK_B), CHUNK_B ** -0.5),
        'sg_b': 1.0 + nrm(ks[12], (DEPTH, G_B, CHUNK_B), 0.02),
        'ml_conv_w': nrm(ks[13], (DEPTH, CONV_W, 2 * D_C), CONV_W ** -0.5),
        'ml_conv_b': nrm(ks[14], (DEPTH, 2 * D_C), 0.02),
        'ml_b_i': nrm(ks[15], (DEPTH, NH_C), 0.1),
        'ml_b_f': jnp.linspace(3.0, 6.0, NH_C, dtype=f32)[None, :] + nrm(ks[16], (DEPTH, NH_C), 0.1),
        'ml_norm_g': 1.0 + nrm(ks[17], (DEPTH, D_C), 0.02),
        'w_br_a': nrm(ks[18], (DEPTH, D_A, D_MODEL), BETA * D_A ** -0.5),
        'w_br_b': nrm(ks[19], (DEPTH, D_B, D_MODEL), BETA * D_B ** -0.5),
        'w_br_c': nrm(ks[20], (DEPTH, D_C, D_MODEL), BETA * D_C ** -0.5),
        'w_out': nrm(ks[21], (DEPTH, D_MODEL, D_MODEL), BETA * D_MODEL ** -0.5),
        'ple_w': nrm(ks[22], (DEPTH, PLE_DIM, D_MODEL), BETA * PLE_DIM ** -0.5),
        'ple_gate': nrm(ks[23], (DEPTH, D_MODEL, D_MODEL), D_MODEL ** -0.5),
        'ln_g': 1.0 + nrm(ks[24], (DEPTH, D_MODEL), 0.02),
        'ln_b': nrm(ks[25], (DEPTH, D_MODEL), 0.02),
    }


def reference(x, p, w_in, cmp_pos_k, cmp_pos_v, cmp_wk1, cmp_wk2, cmp_wv1, cmp_wv2,
              sg_ln_g, sg_ln_b, sg_w, sg_b, ml_conv_w, ml_conv_b, ml_b_i, ml_b_f, ml_norm_g,
              w_br_a, w_br_b, w_br_c, w_out, ple_w, ple_gate, ln_g, ln_b):
    bsz, t_len = x.shape[0], x.shape[1]
    split_at = np.cumsum(IN_SIZES)[:-1].tolist()
    for i in range(DEPTH):
        (a_q, a_kc, a_vc, a_ks, a_vs, a_kw, a_vw, a_g, a_z,
         b_u, b_v, b_z,
         c_q, c_k, c_v, c_if, c_o, c_z, m_g) = jnp.split(x @ w_in[i], split_at, axis=-1)
        y_a = _nsa(a_q, a_kc, a_vc, a_ks, a_vs, a_kw, a_vw, a_g, cmp_pos_k[i], cmp_pos_v[i],
                   cmp_wk1[i], cmp_wk2[i], cmp_wv1[i], cmp_wv2[i]) * jax.nn.silu(a_z)
        y_b = _spatial_gating(b_u, b_v, sg_ln_g[i], sg_ln_b[i], sg_w[i], sg_b[i]) * jax.nn.silu(b_z)
        y_c = _mlstm(c_q, c_k, c_v, c_if, c_o, ml_conv_w[i], ml_conv_b[i], ml_b_i[i], ml_b_f[i],
                     ml_norm_g[i]) * jax.nn.silu(c_z)
        g = jax.nn.sigmoid(m_g).reshape(bsz, t_len, N_BRANCH, D_MODEL)
        merged = g[:, :, 0] * (y_a @ w_br_a[i]) + g[:, :, 1] * (y_b @ w_br_b[i]) + g[:, :, 2] * (y_c @ w_br_c[i])
        r = ALPHA * x + merged @ w_out[i]
        r = r + jax.nn.sigmoid(r @ ple_gate[i]) * (p[i] @ ple_w[i])
        x = _layernorm(r, ln_g[i], ln_b[i])
    return x
```

```python
import numpy as np
from contextlib import ExitStack
import concourse.bass as bass
import concourse.mybir as mybir
from concourse.bass_utils import run_bass_kernel_spmd

F32 = mybir.dt.float32
BF16 = mybir.dt.bfloat16
AF = mybir.ActivationFunctionType
ALU = mybir.AluOpType
AX = mybir.AxisListType

T = 2048
D = 1024
NT = 16
NTB = 4
DEPTH = 2
ALPHA = (2.0 * DEPTH) ** 0.25
LN_EPS = 1e-5
NEG = -30000.0
BIGS = 1.0e9
IN_SIZES = (512, 128, 128, 128, 128, 128, 128, 24, 512, 512, 512, 512, 512, 512, 512, 8, 512, 512, 3072)
_names = ["a_q", "a_kc", "a_vc", "a_ks", "a_vs", "a_kw", "a_vw", "a_g", "a_z", "b_u", "b_v", "b_z",
          "c_q", "c_k", "c_v", "c_if", "c_o", "c_z", "m_g"]
COL = {}
_o = 0
for _n, _s in zip(_names, IN_SIZES):
    COL[_n] = _o
    _o += _s
N_IN = _o


class Tile:
    def __init__(self, h, name):
        self.h = h
        self.name = name
        self.w = {}
        self.r = {}

    def __getitem__(self, k):
        return self.h[k]


class Eng:
    def __init__(self, key, obj, sem):
        self.key, self.obj, self.sem = key, obj, sem
        self.count = 0
        self.seen = {}


class DSem:
    def __init__(self, kb):
        self.kb = kb
        self.subs = {}

    def sub(self, qk):
        cls = "sw" if qk == "pool" else "hw"
        if cls not in self.subs:
            self.subs[cls] = self.kb._newsem()
        return self.subs[cls]


class _Sem:
    def __init__(self, key, h):
        self.key, self.h, self.count = key, h, 0


class KB:
    def __init__(self, nc):
        self.nc = nc
        self.es = ExitStack()
        self.sems = {}
        self.eng = {}
        for key, obj in [("pe", nc.tensor), ("act", nc.scalar), ("dve", nc.vector), ("pool", nc.gpsimd), ("sp", nc.sync)]:
            sem = self.es.enter_context(nc.semaphore("s_" + key))
            self.sems[key] = sem
            self.eng[key] = Eng(key, obj, sem)
        self.nds = 0
        self.ntile = 0
        self.scopes = []
        self.barrier = {}
        self.rot = {}
        self.rotn = {}

    def push(self):
        self.scopes.append((ExitStack(), []))

    def pop(self):
        es, tiles = self.scopes.pop()
        for t in tiles:
            for k, v in list(t.w.items()) + list(t.r.items()):
                self.barrier[k] = max(self.barrier.get(k, 0), v)
        es.close()

    def sbuf(self, name, shape, dt):
        self.ntile += 1
        if self.scopes:
            es, tiles = self.scopes[-1]
        else:
            es, tiles = self.es, []
        t = Tile(es.enter_context(self.nc.sbuf_tensor(f"{name}_{self.ntile}", list(shape), dt)), name)
        t.w = dict(self.barrier)
        tiles.append(t)
        return t

    def psum(self, name, shape, dt):
        self.ntile += 1
        return Tile(self.es.enter_context(self.nc.psum_tensor(f"{name}_{self.ntile}", list(shape), dt)), name)

    def dram(self, name, shape, dt):
        return Tile(self.nc.dram_tensor(name, list(shape), dt, kind="Internal").ap(), name)

    def _newsem(self):
        self.nds += 1
        key = f"q{self.nds}"
        h = self.es.enter_context(self.nc.semaphore("s_" + key))
        self.sems[key] = h
        return _Sem(key, h)

    def dsem(self):
        return DSem(self)

    def _wait(self, e, reads, writes):
        needs = {}
        for t in reads:
            for k, v in t.w.items():
                needs[k] = max(needs.get(k, 0), v)
        for t in writes:
            for k, v in list(t.w.items()) + list(t.r.items()):
                needs[k] = max(needs.get(k, 0), v)
        for k, v in needs.items():
            if e.key == "pe" and k == "pe":
                continue
            if e.seen.get(k, 0) < v:
                e.obj.wait_ge(self.sems[k], v)
                e.seen[k] = v

    def op(self, ek, fn, reads=(), writes=()):
        e = self.eng[ek]
        self._wait(e, reads, writes)
        ins = fn(e.obj)
        e.count += 1
        ins.then_inc(e.sem, 1)
        for t in reads:
            t.r[ek] = e.count
        for t in writes:
            t.w = {ek: e.count}
            t.r = {}
            t.sib = set()
        return ins

    def dma(self, qk, out_ap, in_ap, ds=None, reads=(), writes=(), part=False, **kw):
        e = self.eng[qk]
        cls = "sw" if qk == "pool" else "hw"
        pool = self.rot.setdefault(cls, [])
        if len(pool) < 24:
            pool.append(self._newsem())
            ds = pool[-1]
        else:
            self.rotn[cls] = (self.rotn.get(cls, 0) + 1) % 24
            ds = pool[self.rotn[cls]]
        needs_w = []
        firsts = []
        for t in writes:
            sib = getattr(t, "sib", set())
            if part:
                nonsib = {k: v for k, v in t.w.items() if k not in sib}
                firsts.append(bool(nonsib) or bool(t.r) or not t.w)
                t2 = Tile(None, "")
                t2.w = nonsib
                t2.r = t.r
                needs_w.append(t2)
            else:
                needs_w.append(t)
        self._wait(e, reads, needs_w)
        if ds.count and e.seen.get(ds.key, 0) < ds.count:
            e.obj.wait_ge(ds.h, ds.count)
            e.seen[ds.key] = ds.count
        ins = e.obj.dma_start(out=out_ap, in_=in_ap, **kw)
        ds.count += 16
        ins.then_inc(ds.h, 16)
        for t in reads:
            t.r[ds.key] = ds.count
        for n_, t in enumerate(writes):
            if part and not firsts[n_]:
                t.w[ds.key] = ds.count
                t.sib.add(ds.key)
            else:
                t.w = {ds.key: ds.count}
                t.sib = {ds.key} if part else set()
            t.r = {}
        return ins

    def finish(self, tiles):
        e = self.eng["sp"]
        self._wait(e, tiles, ())


def _consts():
    c = {}
    c["ident"] = np.eye(128, dtype=np.float32)
    s = np.arange(128)[:, None]
    t = np.arange(128)[None, :]
    c["tri"] = (s <= t).astype(np.float32)
    c["ones"] = np.ones((128, 128), np.float32)
    c["cdiag"] = np.where(s <= t, 0.0, NEG).astype(np.float32)
    c["cfar"] = np.where(t < s, 0.0, NEG).astype(np.float32)
    c["masks"] = np.where(s <= t, 128.0 ** -0.5, 0.0).astype(np.float32)
    n = np.arange(127)[:, None]
    tt = np.arange(T)[None, :]
    c["maskc"] = np.where(tt >= 16 * n + 31, 0.0, NEG).astype(np.float32)
    b = np.arange(32)[:, None]
    c["eexp"] = ((tt // 64) == b).astype(np.float32)
    cs = 16 * np.arange(127)[:, None]
    ss = 64 * np.arange(32)[None, :]
    ov = ((cs < ss + 64) & (cs + 32 > ss)).astype(np.float32)
    c["ovaug"] = np.concatenate([ov, np.ones((127, 1), np.float32)], 1)
    slopes = 2.0 ** (-np.arange(1, 9, dtype=np.float64))
    sl = np.arange(128)[:, None, None]
    dl = np.arange(17)[None, None, :]
    c["alb"] = (slopes[None, :, None] * (sl - 127 - 128 * dl)).astype(np.float32).reshape(128, 8 * 17)
    nn = np.arange(127)[:, None, None]
    ti = np.arange(16)[None, None, :]
    c["albc"] = (slopes[None, :, None] * (16 * nn + 31 - (128 * ti + 127))).astype(np.float32).reshape(127, 8 * 16)
    tpos = np.arange(T)[:, None]
    cur = tpos // 64
    jb = np.arange(32)[None, :]
    forced = (jb == 0) | (jb == cur) | (jb == cur - 1)
    allowed = jb <= cur
    keep = (allowed & ~forced).astype(np.float32)
    fneg = np.where(~allowed, -BIGS, np.where(forced, BIGS, 0.0)).astype(np.float32)
    negm = np.where(~allowed, NEG, 0.0).astype(np.float32)

    def tm(a):
        return np.ascontiguousarray(a.reshape(16, 128, 32).transpose(1, 0, 2).reshape(128, 16 * 32))
    c["keep"] = tm(keep)
    c["fneg"] = tm(fneg)
    c["negm"] = tm(negm)
    return c


CONSTS = _consts()
A_STOP = 99


def build(stages=("B", "D"), dbg=(), nlayers=DEPTH):
    nc = bass.Bass("TRN2", target_bir_lowering=False)
    kb = KB(nc)
    x_d = nc.dram_tensor("x", [T, D], F32, kind="ExternalInput").ap()
    p_d = nc.dram_tensor("p", [DEPTH, T, 256], F32, kind="ExternalInput").ap()
    shapes = dict(w_in=[DEPTH, D, N_IN], cmp_pos_k=[DEPTH, 32, 64], cmp_pos_v=[DEPTH, 32, 64],
                  cmp_wk1=[DEPTH, 2048, 128], cmp_wk2=[DEPTH, 128, 64], cmp_wv1=[DEPTH, 2048, 128],
                  cmp_wv2=[DEPTH, 128, 64], sg_ln_g=[DEPTH, 512], sg_ln_b=[DEPTH, 512],
                  sg_w=[DEPTH, 4, 128, 128], sg_b=[DEPTH, 4, 128], ml_conv_w=[DEPTH, 4, 1024],
                  ml_conv_b=[DEPTH, 1024], ml_b_i=[DEPTH, 4], ml_b_f=[DEPTH, 4], ml_norm_g=[DEPTH, 512],
                  w_br_a=[DEPTH, 512, D], w_br_b=[DEPTH, 512, D], w_br_c=[DEPTH, 512, D],
                  w_out=[DEPTH, D, D], ple_w=[DEPTH, 256, D], ple_gate=[DEPTH, D, D],
                  ln_g=[DEPTH, D], ln_b=[DEPTH, D])
    W = {k: nc.dram_tensor(k, s, F32, kind="ExternalInput").ap() for k, s in shapes.items()}
    CD = {k: nc.dram_tensor("c_" + k, list(v.shape), F32, kind="ExternalInput").ap() for k, v in CONSTS.items()}
    out_d = Tile(nc.dram_tensor("out", [T, D], F32, kind="ExternalOutput").ap(), "out")
    x1_d = kb.dram("x1_scratch", [T, D], F32)
    dbg_d = {}
    ds_out = kb.dsem()

    def dbg_out(name, tile, ap, shape, dt=F32):
        if name not in dbg:
            return
        d = Tile(nc.dram_tensor("dbg_" + name, list(shape), dt, kind="ExternalOutput").ap(), name)
        dbg_d[name] = d
        kb.dma("sp", d.h, ap, ds_out, reads=[tile], writes=[d])

    XT = [kb.sbuf(f"xT{i}", [128, 8, 512], BF16) for i in range(NTB)]
    YT = {}
    WSD = [kb.dsem() for _ in range(3)]
    WS = []
    wsi = [0]

    def alloc_ws():
        WS.clear()
        WS.extend(kb.sbuf(f"ws{i}", [128, 8, 640], BF16) for i in range(3))
    PS = [kb.psum(f"ps{i}", [128, 512], F32) for i in range(8)]
    ident_b = kb.sbuf("ident_b", [128, 128], BF16)
    ident_f = kb.sbuf("ident_f", [128, 128], F32)
    ds_c = kb.dsem()
    kb.dma("pool", ident_b[:, :], CD["ident"], ds_c, writes=[ident_b])
    kb.dma("sp", ident_f[:, :], CD["ident"], ds_c, writes=[ident_f])

    def wslot():
        i = wsi[0] % 3
        wsi[0] += 1
        return WS[i], WSD[i]

    def load_w(src_ap, ncols, nk=8, col0=0, slot=None):
        ws, wd = slot if slot is not None else wslot()
        kb.dma("pool", ws[:, 0:nk, col0:col0 + ncols], src_ap.rearrange("(kc p) n -> p kc n", p=128), wd, writes=[ws], part=(slot is not None))
        return ws, wd

    def bcast_load(name, src_row_ap, n, dt=F32, q="sp"):
        t = kb.sbuf(name, [128, n], dt)
        kb.dma(q, t[:, :], src_row_ap.partition_broadcast(128), ds_c, writes=[t])
        return t

    def mm(ps_tile, out_ap, pairs, reads, start=True, stop=True):
        def fn(pe):
            n = len(pairs)
            ins = None
            for i, (l, r) in enumerate(pairs):
                ins = pe.matmul(out_ap, l, r, start=(start and i == 0), stop=(stop and i == n - 1))
            return ins
        return kb.op("pe", fn, reads=reads, writes=[ps_tile])

    def rstd_(s_, c, eps=LN_EPS):
        kb.op("dve", lambda v: v.tensor_scalar_add(s_[:, c:c + 1], s_[:, c:c + 1], eps), reads=[s_], writes=[s_])
        kb.op("act", lambda a: a.activation(out=s_[:, c:c + 1], in_=s_[:, c:c + 1], func=AF.Sqrt), reads=[s_], writes=[s_])
        kb.op("dve", lambda v: v.reciprocal(s_[:, c:c + 1], s_[:, c:c + 1]), reads=[s_], writes=[s_])

    def transpose_to(ps_tile, out_ap, in_ap, reads, ident):
        return kb.op("pe", lambda pe: pe.transpose(out_ap, in_ap, ident), reads=reads, writes=[ps_tile])

    kb.push()
    xs = [kb.sbuf(f"xs{i}", [128, D], BF16) for i in range(2)]
    xsd = [kb.dsem() for _ in range(2)]

    def tok_to_XT(src_tile, src, i, psn):
        ps = PS[psn]
        pb = ps[:, :].bitcast(BF16)
        def fn(pe):
            ins = None
            for kc in range(8):
                ins = pe.transpose(pb[:, kc * 128:(kc + 1) * 128], src[:, kc * 128:(kc + 1) * 128], ident_b[:, :])
            return ins
        kb.op("pe", fn, reads=[src_tile, ident_b], writes=[ps])
        dst = XT[i // 4]
        kb.op("dve", lambda v: v.tensor_copy(dst[:, :, (i % 4) * 128:(i % 4 + 1) * 128],
                                             pb.rearrange("p (k t) -> p k t", k=8)),
              reads=[ps], writes=[dst])

    for i in range(NT):
        s, sd = xs[i % 2], xsd[i % 2]
        kb.dma("pool", s[:, :], x_d[i * 128:(i + 1) * 128, :], sd, writes=[s])
        tok_to_XT(s, s, i, i % 2)
    kb.pop()

    for L in range(nlayers):
        res_src = Tile(x_d, "xres") if L == 0 else x1_d
        dst_d = out_d if L == nlayers - 1 else x1_d
        w_in = W["w_in"][L]
        kb.push()
        YT.clear()

        if "A" in stages:
            YT["a"] = [kb.sbuf(f"yTa{i}", [128, 4, 512], BF16) for i in range(NTB)]
            kb.push()
            alloc_ws()
            SL = [2.0 ** (-(h_ + 1)) for h_ in range(8)]
            cdiag_b = kb.sbuf("cdiag_b", [128, 128], BF16)
            cfar_b = kb.sbuf("cfar_b", [128, 128], BF16)
            maskc_b = kb.sbuf("maskc_b", [127, T], BF16)
            alb = kb.sbuf("alb", [128, 8 * 17], F32)
            albc = kb.sbuf("albc", [127, 8 * 16], F32)
            keep_t = kb.sbuf("keep_t", [128, 512], F32)
            fneg_t = kb.sbuf("fneg_t", [128, 512], F32)
            negm_t = kb.sbuf("negm_t", [128, 512], F32)
            kb.dma("pool", cdiag_b[:, :], CD["cdiag"], ds_c, writes=[cdiag_b])
            kb.dma("pool", cfar_b[:, :], CD["cfar"], ds_c, writes=[cfar_b])
            kb.dma("pool", maskc_b[:, :], CD["maskc"], ds_c, writes=[maskc_b])
            kb.dma("sp", alb[:, :], CD["alb"], ds_c, writes=[alb])
            kb.dma("sp", albc[:, :], CD["albc"], ds_c, writes=[albc])
            kb.dma("sp", keep_t[:, :], CD["keep"], ds_c, writes=[keep_t])
            kb.dma("sp", fneg_t[:, :], CD["fneg"], ds_c, writes=[fneg_t])
            kb.dma("sp", negm_t[:, :], CD["negm"], ds_c, writes=[negm_t])
            w1k = kb.sbuf("w1k", [64, 32, 128], BF16)
            w1v = kb.sbuf("w1v", [64, 32, 128], BF16)
            w2k = kb.sbuf("w2k", [128, 64], BF16)
            w2v = kb.sbuf("w2v", [128, 64], BF16)
            posk = kb.sbuf("posk", [64, 32], BF16)
            posv = kb.sbuf("posv", [64, 32], BF16)
            kb.dma("pool", w1k[:, :, :], W["cmp_wk1"][L].rearrange("(l d) h -> d l h", d=64), ds_c, writes=[w1k])
            kb.dma("pool", w1v[:, :, :], W["cmp_wv1"][L].rearrange("(l d) h -> d l h", d=64), ds_c, writes=[w1v])
            kb.dma("pool", w2k[:, :], W["cmp_wk2"][L], ds_c, writes=[w2k])
            kb.dma("pool", w2v[:, :], W["cmp_wv2"][L], ds_c, writes=[w2v])
            kb.dma("pool", posk[:, :], W["cmp_pos_k"][L].rearrange("l d -> d l"), ds_c, writes=[posk], allow_slow_non_contiguous=True)
            kb.dma("pool", posv[:, :], W["cmp_pos_v"][L].rearrange("l d -> d l"), ds_c, writes=[posv], allow_slow_non_contiguous=True)
            cvk = kb.sbuf("cvk", [128, 1], F32)
            cvv = kb.sbuf("cvv", [128, 1], F32)
            for w1_, pos_, cv_ in ((w1k, posk, cvk), (w1v, posv, cvv)):
                ps = PS[0]
                mm(ps, ps[:, 0:1], [(w1_[:, l, :], pos_[:, l:l + 1]) for l in range(32)], reads=[w1_, pos_])
                kb.op("dve", lambda v, cv_=cv_, ps=ps: v.tensor_copy(cv_[:, :], ps[:, 0:1]), reads=[ps], writes=[cv_])
            wg, _ = load_w(w_in[:, COL["a_g"]:COL["a_g"] + 24], 24)
            gates = kb.sbuf("gates", [128, 16, 24], F32)
            ps = PS[1]
            def fngate(pe):
                ins = None
                for i in range(NT):
                    for kc in range(8):
                        ins = pe.matmul(ps[:, i * 24:(i + 1) * 24], XT[i // 4][:, kc, (i % 4) * 128:(i % 4 + 1) * 128], wg[:, kc, 0:24], start=(kc == 0), stop=(kc == 7))
                return ins
            kb.op("pe", fngate, reads=XT + [wg], writes=[ps])
            kb.op("act", lambda a: a.activation(out=gates[:, :, :], in_=ps[:, 0:384].rearrange("p (c e) -> p c e", e=24), func=AF.Sigmoid), reads=[ps], writes=[gates])

            QA = [kb.sbuf(f"qaug{i}", [96, T], BF16) for i in range(4)]
            kcT = kb.sbuf("kcT", [64, T], BF16)
            vcT = kb.sbuf("vcT", [64, T], BF16)
            ksA = kb.sbuf("ksaug", [96, T], BF16)
            kwT = kb.sbuf("kwT", [64, T], BF16)
            vsA = kb.sbuf("vsaug", [128, 16, 66], BF16)
            vwA = kb.sbuf("vwaug", [128, 16, 66], BF16)
            kb.dma("pool", ksA[64:96, :], CD["eexp"], ds_c, writes=[ksA])
            kb.op("dve", lambda v: v.memset(vsA[:, :, 64:65], 1.0), writes=[vsA])
            kb.op("dve", lambda v: v.memset(vwA[:, :, 64:65], 1.0), writes=[vwA])
            kcmpT = kb.sbuf("kcmpT", [64, 127], BF16)
            vcA = kb.sbuf("vcaug", [127, 97], BF16)
            kb.dma("pool", vcA[:, 64:97], CD["ovaug"], ds_c, writes=[vcA])
            hk = kb.sbuf("hk", [128, 127], BF16)
            hv = kb.sbuf("hv", [128, 127], BF16)
            oacc = kb.sbuf("oacc", [128, 16, 256], F32)
            imp = kb.sbuf("imp", [128, 16, 32], F32)
            pTc = [kb.sbuf(f"pTc{i}", [127, 512], BF16) for i in range(2)]
            pTs = [kb.sbuf(f"pTs{i}", [128, 512], BF16) for i in range(2)]
            pTw = [kb.sbuf(f"pTw{i}", [128, 384], BF16) for i in range(2)]
            nTs = [kb.sbuf(f"nT{i}", [65, 512], F32) for i in range(2)]
            rds = [kb.sbuf(f"rd{i}", [128, 8], F32) for i in range(2)]
            scs = [kb.sbuf(f"scr{i}", [128, 32], F32) for i in range(2)]
            top8 = [kb.sbuf(f"top8{i}", [128, 8], F32) for i in range(2)]
            nmb = [kb.sbuf(f"nmb{i}", [128, 96], BF16) for i in range(4)]
            for t_ in nmb:
                kb.op("dve", lambda v, t_=t_: v.memset(t_[:, :], 0.0), writes=[t_])
            szs = [kb.sbuf(f"asz{i}", [128, 256], F32) for i in range(2)]
            yas = [kb.sbuf(f"aya{i}", [128, 256], BF16) for i in range(2)]
            pn = 0
            for g in range(2):
                ws, wd = wslot()
                c0 = 0
                for nm, wdt in (("a_q", 256), ("a_kc", 64), ("a_vc", 64), ("a_ks", 64), ("a_kw", 64), ("a_vs", 64), ("a_vw", 64)):
                    load_w(w_in[:, COL[nm] + g * wdt:COL[nm] + (g + 1) * wdt], wdt, col0=c0, slot=(ws, wd))
                    c0 += wdt
                wz, _ = load_w(w_in[:, COL["a_z"] + g * 256:COL["a_z"] + (g + 1) * 256], 256)
                fm = [(QA[hl], hl * 64, 0.125) for hl in range(4)] + [(kcT, 256, 1.0), (vcT, 320, 1.0), (ksA, 384, 1.0), (kwT, 448, 1.0)]
                for dst, cofs, scl in fm:
                    for tb in range(NTB):
                        ps = PS[pn % 2]
                        pn += 1
                        mm(ps, ps[0:64, :], [(ws[:, kc, cofs:cofs + 64], XT[tb][:, kc, :]) for kc in range(8)], reads=[ws, XT[tb]])
                        if scl != 1.0:
                            kb.op("act", lambda a, dst=dst, ps=ps, tb=tb, scl=scl: a.mul(out=dst[0:64, tb * 512:(tb + 1) * 512], in_=ps[0:64, :], mul=scl), reads=[ps], writes=[dst])
                        else:
                            kb.op("dve", lambda v, dst=dst, ps=ps, tb=tb: v.tensor_copy(dst[0:64, tb * 512:(tb + 1) * 512], ps[0:64, :]), reads=[ps], writes=[dst])
                if A_STOP <= 1:
                    break
                for i in range(NT):
                    ps = PS[pn % 2]
                    pn += 1
                    tsl = slice((i % 4) * 128, (i % 4 + 1) * 128)
                    mm(ps, ps[:, 0:128], [(XT[i // 4][:, kc, tsl], ws[:, kc, 512:640]) for kc in range(8)], reads=[ws, XT[i // 4]])
                    kb.op("dve", lambda v, ps=ps, i=i: v.tensor_copy(vsA[:, i, 0:64], ps[:, 0:64]), reads=[ps], writes=[vsA])
                    kb.op("dve", lambda v, ps=ps, i=i: v.tensor_copy(vwA[:, i, 0:64], ps[:, 64:128]), reads=[ps], writes=[vwA])
                if A_STOP <= 2:
                    break
                for src, w1_, cv_, h_ in ((kcT, w1k, cvk, hk), (vcT, w1v, cvv, hv)):
                    ps = PS[pn % 2]
                    pn += 1
                    mm(ps, ps[:, 0:127], [(w1_[:, l, :], src[:, l:l + 2017:16]) for l in range(32)], reads=[w1_, src])
                    kb.op("act", lambda a, h_=h_, ps=ps, cv_=cv_: a.activation(out=h_[:, :], in_=ps[:, 0:127], func=AF.Gelu, bias=cv_[:, 0:1]), reads=[ps, cv_], writes=[h_])
                ps = PS[pn % 2]
                pn += 1
                mm(ps, ps[0:64, 0:127], [(w2k[:, :], hk[:, :])], reads=[w2k, hk])
                kb.op("dve", lambda v, ps=ps: v.tensor_copy(kcmpT[:, :], ps[0:64, 0:127]), reads=[ps], writes=[kcmpT])
                ps = PS[pn % 2]
                pn += 1
                mm(ps, ps[0:127, 0:64], [(hv[:, :], w2v[:, :])], reads=[w2v, hv])
                kb.op("dve", lambda v, ps=ps: v.tensor_copy(vcA[:, 0:64], ps[0:127, 0:64]), reads=[ps], writes=[vcA])
                if A_STOP <= 3:
                    break
                for hl in range(4):
                    h = g * 4 + hl
                    for tb in range(NTB):
                        pss, pso = PS[2 + tb % 2], PS[4 + tb % 2]
                        pT = pTc[tb % 2]
                        tbs = slice(tb * 512, (tb + 1) * 512)
                        def fnc(pe, pss=pss, hl=hl, tbs=tbs):
                            pe.matmul(pss[0:127, :], kcmpT[:, :], QA[hl][0:64, tbs], start=True, stop=False)
                            return pe.matmul(pss[0:127, :], ident_b[0:127, 0:127], maskc_b[:, tbs], start=False, stop=True)
                        kb.op("pe", fnc, reads=[kcmpT, QA[hl], ident_b, maskc_b], writes=[pss])
                        for il in range(4):
                            i = tb * 4 + il
                            kb.op("act", lambda a, pT=pT, pss=pss, il=il, h=h, i=i: a.activation(out=pT[:, il * 128:(il + 1) * 128], in_=pss[0:127, il * 128:(il + 1) * 128], func=AF.Exp, bias=albc[:, h * 16 + i:h * 16 + i + 1]), reads=[pss, albc], writes=[pT])
                        def fno(pe, pso=pso, pT=pT):
                            ins = None
                            for il in range(4):
                                ins = pe.matmul(pso[:, il * 97:(il + 1) * 97], pT[:, il * 128:(il + 1) * 128], vcA[:, :], start=True, stop=True)
                            return ins
                        kb.op("pe", fno, reads=[pT, vcA], writes=[pso])
                        rd = rds[tb % 2]
                        kb.op("dve", lambda v, rd=rd, pso=pso: v.tensor_scalar_max(rd[:, 0:4], pso[:, 96:388:97], 1e-37), reads=[pso], writes=[rd])
                        kb.op("dve", lambda v, rd=rd: v.reciprocal(rd[:, 0:4], rd[:, 0:4]), reads=[rd], writes=[rd])
                        kb.op("dve", lambda v, rd=rd, tb=tb, h=h: v.tensor_tensor(rd[:, 4:8], rd[:, 0:4], gates[:, tb * 4:(tb + 1) * 4, h], op=ALU.mult), reads=[rd, gates], writes=[rd])
                        for il in range(4):
                            i = tb * 4 + il
                            kb.op("dve", lambda v, pso=pso, rd=rd, il=il, i=i, hl=hl: v.tensor_scalar_mul(oacc[:, i, hl * 64:(hl + 1) * 64], pso[:, il * 97:il * 97 + 64], rd[:, 4 + il:5 + il]), reads=[pso, rd], writes=[oacc])
                            if hl == 0:
                                kb.op("dve", lambda v, pso=pso, rd=rd, il=il, i=i: v.tensor_scalar_mul(imp[:, i, :], pso[:, il * 97 + 64:il * 97 + 96], rd[:, il:il + 1]), reads=[pso, rd], writes=[imp])
                            else:
                                kb.op("dve", lambda v, pso=pso, rd=rd, il=il, i=i: v.scalar_tensor_tensor(imp[:, i, :], pso[:, il * 97 + 64:il * 97 + 96], rd[:, il:il + 1], imp[:, i, :], op0=ALU.mult, op1=ALU.add), reads=[pso, rd, imp], writes=[imp])
                if A_STOP <= 4:
                    break
                for tb in range(NTB):
                    pst = PS[6 + tb % 2]
                    for il in range(4):
                        i = tb * 4 + il
                        sc, t8, nm_ = scs[i % 2], top8[i % 2], nmb[il]
                        kb.op("dve", lambda v, sc=sc, i=i: v.tensor_tensor(sc[:, :], imp[:, i, :], keep_t[:, i * 32:(i + 1) * 32], op=ALU.mult), reads=[imp, keep_t], writes=[sc])
                        kb.op("dve", lambda v, sc=sc, i=i: v.tensor_tensor(sc[:, :], sc[:, :], fneg_t[:, i * 32:(i + 1) * 32], op=ALU.add), reads=[sc, fneg_t], writes=[sc])
                        kb.op("dve", lambda v, sc=sc, t8=t8: v.max(t8[:, :], sc[:, :]), reads=[sc], writes=[t8])
                        kb.op("dve", lambda v, sc=sc, t8=t8: v.tensor_scalar(sc[:, :], sc[:, :], t8[:, 7:8], NEG, op0=ALU.is_lt, op1=ALU.mult), reads=[sc, t8], writes=[sc])
                        kb.op("dve", lambda v, sc=sc, nm_=nm_, i=i: v.tensor_tensor(nm_[:, 64:96], sc[:, :], negm_t[:, i * 32:(i + 1) * 32], op=ALU.add), reads=[sc, negm_t], writes=[nm_])
                        mm(pst, pst[0:96, il * 128:(il + 1) * 128], [(nm_[:, :], ident_b[:, :])], reads=[nm_, ident_b])
                    for hl in range(4):
                        eng_ = "dve"
                        if eng_ == "act":
                            kb.op("act", lambda a, hl=hl, pst=pst, tb=tb: a.copy(out=QA[hl][64:96, tb * 512:(tb + 1) * 512], in_=pst[64:96, :]), reads=[pst], writes=[QA[hl]])
                        else:
                            kb.op("dve", lambda v, hl=hl, pst=pst, tb=tb: v.tensor_copy(QA[hl][64:96, tb * 512:(tb + 1) * 512], pst[64:96, :]), reads=[pst], writes=[QA[hl]])

                if A_STOP <= 5:
                    break
                def finish_branch(pspv, hl, h, qg, gidx, n_):
                    nT, rd, psf = nTs[n_ % 2], rds[n_ % 2], PS[6 + n_ % 2]
                    kb.op("act", lambda a: a.copy(out=nT[:, :], in_=pspv[0:65, :]), reads=[pspv], writes=[nT])
                    def fnf(pe):
                        ins = None
                        for il in range(4):
                            ins = pe.matmul(psf[:, il * 65:(il + 1) * 65], nT[:, il * 128:(il + 1) * 128], ident_f[0:65, 0:65], start=True, stop=True)
                        return ins
                    kb.op("pe", fnf, reads=[nT, ident_f], writes=[psf])
                    kb.op("dve", lambda v: v.tensor_scalar_max(rd[:, 0:4], psf[:, 64:260:65], 1e-37), reads=[psf], writes=[rd])
                    kb.op("dve", lambda v: v.reciprocal(rd[:, 0:4], rd[:, 0:4]), reads=[rd], writes=[rd])
                    kb.op("dve", lambda v: v.tensor_tensor(rd[:, 4:8], rd[:, 0:4], gates[:, qg * 4:(qg + 1) * 4, gidx * 8 + h], op=ALU.mult), reads=[rd, gates], writes=[rd])
                    for il in range(4):
                        i = qg * 4 + il
                        kb.op("dve", lambda v, il=il, i=i: v.scalar_tensor_tensor(oacc[:, i, hl * 64:(hl + 1) * 64], psf[:, il * 65:il * 65 + 64], rd[:, 4 + il:5 + il], oacc[:, i, hl * 64:(hl + 1) * 64], op0=ALU.mult, op1=ALU.add), reads=[psf, rd, oacc], writes=[oacc])

                nfin = 0
                for hl in range(4):
                    h = g * 4 + hl
                    for qg in range(4):
                        pspv = PS[4 + qg % 2]
                        nj = 4 * qg + 4
                        for j in range(nj):
                            jl = j - 4 * qg
                            c0 = max(jl, 0) * 128
                            pss = PS[2 + j % 2]
                            pT = pTs[j % 2]
                            def fns(pe, pss=pss, j=j, jl=jl, c0=c0, qg=qg, hl=hl):
                                ins = pe.matmul(pss[:, c0:512], ksA[0:96, j * 128:(j + 1) * 128], QA[hl][0:96, qg * 512 + c0:(qg + 1) * 512], start=True, stop=(jl < 0))
                                if jl >= 0:
                                    ins = pe.matmul(pss[:, c0:c0 + 128], ident_b[:, :], cdiag_b[:, :], start=False, stop=True)
                                return ins
                            kb.op("pe", fns, reads=[ksA, QA[hl], ident_b, cdiag_b], writes=[pss])
                            for il in range(max(jl, 0), 4):
                                dlt = 4 * qg + il - j
                                kb.op("act", lambda a, pT=pT, pss=pss, il=il, dlt=dlt, h=h: a.activation(out=pT[:, il * 128:(il + 1) * 128], in_=pss[:, il * 128:(il + 1) * 128], func=AF.Exp, bias=alb[:, h * 17 + dlt:h * 17 + dlt + 1]), reads=[pss, alb], writes=[pT])
                            mm(pspv, pspv[0:65, c0:512], [(vsA[:, j, 0:65], pT[:, c0:512])], reads=[vsA, pT], start=(j == 0), stop=(j == nj - 1))
                        finish_branch(pspv, hl, h, qg, 1, nfin)
                        nfin += 1
                    for qg in range(4):
                        pspv = PS[4 + qg % 2]
                        for il in range(4):
                            i = qg * 4 + il
                            js = [j for j in (i - 2, i - 1, i) if j >= 0]
                            psw = PS[2 + il % 2]
                            pT = pTw[il % 2]
                            def fnw(pe, psw=psw, js=js, i=i, hl=hl):
                                ins = None
                                for jj, j in enumerate(js):
                                    msk = cdiag_b if j == i else (cfar_b if j == i - 2 else None)
                                    ins = pe.matmul(psw[:, jj * 128:(jj + 1) * 128], kwT[:, j * 128:(j + 1) * 128], QA[hl][0:64, i * 128:(i + 1) * 128], start=True, stop=(msk is None))
                                    if msk is not None:
                                        ins = pe.matmul(psw[:, jj * 128:(jj + 1) * 128], ident_b[:, :], msk[:, :], start=False, stop=True)
                                return ins
                            kb.op("pe", fnw, reads=[kwT, QA[hl], ident_b, cdiag_b, cfar_b], writes=[psw])
                            for jj, j in enumerate(js):
                                dlt = i - j
                                kb.op("act", lambda a, pT=pT, psw=psw, jj=jj, dlt=dlt, h=h: a.activation(out=pT[:, jj * 128:(jj + 1) * 128], in_=psw[:, jj * 128:(jj + 1) * 128], func=AF.Exp, bias=alb[:, h * 17 + dlt:h * 17 + dlt + 1]), reads=[psw, alb], writes=[pT])
                            mm(pspv, pspv[0:65, il * 128:(il + 1) * 128], [(vwA[:, j, 0:65], pT[:, jj * 128:(jj + 1) * 128]) for jj, j in enumerate(js)], reads=[vwA, pT])
                        finish_branch(pspv, hl, h, qg, 2, nfin)
                        nfin += 1
                if A_STOP <= 7:
                    break
                for i in range(NT):
                    ps = PS[i % 2]
                    tsl = slice((i % 4) * 128, (i % 4 + 1) * 128)
                    sz, ya = szs[i % 2], yas[i % 2]
                    mm(ps, ps[:, 0:256], [(XT[i // 4][:, kc, tsl], wz[:, kc, 0:256]) for kc in range(8)], reads=[wz, XT[i // 4]])
                    kb.op("act", lambda a, sz=sz, ps=ps: a.activation(out=sz[:, :], in_=ps[:, 0:256], func=AF.Silu), reads=[ps], writes=[sz])
                    kb.op("dve", lambda v, sz=sz, ya=ya, i=i: v.tensor_tensor(ya[:, :], oacc[:, i, :], sz[:, :], op=ALU.mult), reads=[oacc, sz], writes=[ya])
                    pst = PS[6 + i % 2]
                    ptb = pst[:, :].bitcast(BF16)
                    def fny(pe, ptb=ptb, ya=ya):
                        pe.transpose(ptb[:, 0:128], ya[:, 0:128], ident_b[:, :])
                        return pe.transpose(ptb[:, 128:256], ya[:, 128:256], ident_b[:, :])
                    kb.op("pe", fny, reads=[ya, ident_b], writes=[pst])
                    ytile = YT["a"][i // 4]
                    kb.op("dve", lambda v, ytile=ytile, ptb=ptb, tsl=tsl: v.tensor_copy(ytile[:, 2 * g:2 * g + 2, tsl], ptb[:, 0:256].rearrange("p (c t) -> p c t", c=2)), reads=[pst], writes=[ytile])
            if L == 0:
                for tb in range(NTB):
                    dbg_out(f"yaT{tb}", YT["a"][tb], YT["a"][tb][:, :, :], [128, 4, 512], BF16)
            kb.pop()

        if "B" in stages:
            YT["b"] = [kb.sbuf(f"yTb{i}", [128, 4, 512], BF16) for i in range(NTB)]
            kb.push()
            alloc_ws()
            lng = bcast_load("sg_lng", W["sg_ln_g"][L], 512)
            lnb = bcast_load("sg_lnb", W["sg_ln_b"][L], 512)
            sgb = bcast_load("sg_bias", W["sg_b"][L].rearrange("g t -> (g t)"), 512)
            wst = kb.sbuf("sg_wT", [128, 4, 128], BF16)
            wraw = kb.sbuf("sg_wraw", [128, 4, 128], F32)
            wmsk = kb.sbuf("sg_wm", [128, 4, 128], BF16)
            tril_ts = kb.sbuf("tril_ts", [128, 128], F32)
            kb.dma("sp", tril_ts[:, :], CD["tri"].rearrange("s t -> t s"), ds_c, writes=[tril_ts], allow_slow_non_contiguous=True) if False else None
            kb.dma("sp", wraw[:, :, :], W["sg_w"][L].rearrange("g t s -> t g s"), ds_c, writes=[wraw])
            trit = kb.sbuf("trit", [128, 128], F32)
            kb.dma("sp", trit[:, :], CD["tri"], ds_c, writes=[trit])
            lowm = kb.sbuf("lowm", [128, 128], F32)
            kb.op("dve", lambda v: v.tensor_scalar(lowm[:, :], trit[:, :], -1.0, 1.0, op0=ALU.mult, op1=ALU.add), reads=[trit], writes=[lowm])
            kb.op("dve", lambda v: v.tensor_tensor(lowm[:, :], lowm[:, :], ident_f[:, :], op=ALU.add), reads=[lowm, ident_f], writes=[lowm])
            for g in range(4):
                kb.op("dve", lambda v, g=g: v.tensor_tensor(wmsk[:, g, :], wraw[:, g, :], lowm[:, :], op=ALU.mult), reads=[wraw, lowm], writes=[wmsk])
            psb = PS[2][:, :].bitcast(BF16)
            def fnT(pe):
                ins = None
                for g in range(4):
                    ins = pe.transpose(psb[:, g * 128:(g + 1) * 128], wmsk[:, g, :], ident_b[:, :])
                return ins
            kb.op("pe", fnT, reads=[wmsk, ident_b], writes=[PS[2]])
            kb.op("dve", lambda v: v.tensor_copy(wst[:, :, :], psb[:, 0:512].rearrange("p (g t) -> p g t", g=4)), reads=[PS[2]], writes=[wst])

            VT = [kb.sbuf(f"vtok{i}", [128, 4, 512], BF16) for i in range(NTB)]
            ws, _ = load_w(w_in[:, COL["b_v"]:COL["b_v"] + 512], 512)
            vg = [kb.sbuf(f"vg{i}", [128, 512], F32) for i in range(2)]
            st = [kb.sbuf(f"vst{i}", [128, 8], F32) for i in range(2)]
            for i in range(NT):
                ps = PS[i % 2]
                xt = XT[i // 4]
                mm(ps, ps[:, :], [(xt[:, kc, (i % 4) * 128:(i % 4 + 1) * 128], ws[:, kc, 0:512]) for kc in range(8)], reads=[xt, ws])
                g_, s_ = vg[i % 2], st[i % 2]
                kb.op("act", lambda a, g_=g_, ps=ps: a.activation(out=g_[:, :], in_=ps[:, :], func=AF.Gelu), reads=[ps], writes=[g_])
                kb.op("dve", lambda v, g_=g_, s_=s_: v.bn_stats(s_[:, 0:6], g_[:, :]), reads=[g_], writes=[s_])
                mv = s_
                kb.op("dve", lambda v, s_=s_: v.bn_aggr(s_[:, 6:8], s_[:, 0:6]), reads=[s_], writes=[s_])
                rstd_(s_, 7)
                kb.op("dve", lambda v, g_=g_, s_=s_: v.tensor_scalar(g_[:, :], g_[:, :], s_[:, 6:7], s_[:, 7:8], op0=ALU.subtract, op1=ALU.mult), reads=[g_, s_], writes=[g_])
                kb.op("dve", lambda v, g_=g_: v.tensor_tensor(g_[:, :], g_[:, :], lng[:, :], op=ALU.mult), reads=[g_, lng], writes=[g_])
                vt = VT[i // 4]
                kb.op("dve", lambda v, g_=g_, vt=vt, i=i: v.tensor_tensor(vt[:, i % 4, :], g_[:, :], lnb[:, :], op=ALU.add), reads=[g_, lnb], writes=[vt])

            UZ = [kb.sbuf(f"uz{i}", [128, 4, 512], BF16) for i in range(NTB)]
            wu, _ = load_w(w_in[:, COL["b_u"]:COL["b_u"] + 512], 512)
            wz, _ = load_w(w_in[:, COL["b_z"]:COL["b_z"] + 512], 512)
            ug = [kb.sbuf(f"ug{i}", [128, 512], BF16) for i in range(2)]
            zg = [kb.sbuf(f"zg{i}", [128, 512], BF16) for i in range(2)]
            n = 0
            for tb in range(NTB):
                xt = XT[tb]
                for fc in range(4):
                    pu, pz = PS[(2 * n) % 4], PS[(2 * n + 1) % 4]
                    u_, z_ = ug[n % 2], zg[n % 2]
                    n += 1
                    mm(pu, pu[:, :], [(wu[:, kc, fc * 128:(fc + 1) * 128], xt[:, kc, :]) for kc in range(8)], reads=[xt, wu])
                    mm(pz, pz[:, :], [(wz[:, kc, fc * 128:(fc + 1) * 128], xt[:, kc, :]) for kc in range(8)], reads=[xt, wz])
                    kb.op("act", lambda a, u_=u_, pu=pu: a.activation(out=u_[:, :], in_=pu[:, :], func=AF.Gelu), reads=[pu], writes=[u_])
                    kb.op("act", lambda a, z_=z_, pz=pz: a.activation(out=z_[:, :], in_=pz[:, :], func=AF.Silu), reads=[pz], writes=[z_])
                    uz = UZ[tb]
                    kb.op("dve", lambda v, uz=uz, fc=fc, u_=u_, z_=z_: v.tensor_tensor(uz[:, fc, :], u_[:, :], z_[:, :], op=ALU.mult), reads=[u_, z_], writes=[uz])

            mt = [kb.sbuf(f"mixt{i}", [128, 512], F32) for i in range(2)]
            n = 0
            for tb in range(NTB):
                vt = VT[tb]
                for g in range(4):
                    ps = PS[4 + n % 2]
                    m_ = mt[n % 2]
                    n += 1
                    def fn(pe, ps=ps, vt=vt, g=g):
                        ins = None
                        for c in range(4):
                            ins = pe.matmul(ps[:, c * 128:(c + 1) * 128], vt[:, c, g * 128:(g + 1) * 128], wst[:, g, :], start=True, stop=True)
                        return ins
                    kb.op("pe", fn, reads=[vt, wst], writes=[ps])
                    bias_ap = sgb[:, g * 128:(g + 1) * 128].unsqueeze(1).to_broadcast([128, 4, 128])
                    kb.op("dve", lambda v, m_=m_, ps=ps, bias_ap=bias_ap: v.tensor_tensor(m_[:, :].rearrange("p (c t) -> p c t", c=4), ps[:, :].rearrange("p (c t) -> p c t", c=4), bias_ap, op=ALU.add), reads=[ps, sgb], writes=[m_])
                    yt = YT["b"][tb]
                    uz = UZ[tb]
                    kb.op("dve", lambda v, yt=yt, g=g, m_=m_, uz=uz: v.tensor_tensor(yt[:, g, :], m_[:, :], uz[:, g, :], op=ALU.mult), reads=[m_, uz], writes=[yt])
            if L == 0:
                for tb in range(NTB):
                    dbg_out(f"ybT{tb}", YT["b"][tb], YT["b"][tb][:, :, :], [128, 4, 512], BF16)
            kb.pop()

        if "C" in stages:
            YT["c"] = [kb.sbuf(f"yTc{i}", [128, 4, 512], BF16) for i in range(NTB)]
            kb.push()
            alloc_ws()
            trif = kb.sbuf("trif", [128, 128], F32)
            onesf = kb.sbuf("onesf", [128, 128], F32)
            maskS = kb.sbuf("maskS", [128, 128], F32)
            kb.dma("sp", trif[:, :], CD["tri"], ds_c, writes=[trif])
            kb.dma("sp", onesf[:, :], CD["ones"], ds_c, writes=[onesf])
            kb.dma("sp", maskS[:, :], CD["masks"], ds_c, writes=[maskS])
            bi_bc = bcast_load("ml_bi", W["ml_b_i"][L], 4)
            bf_bc = bcast_load("ml_bf", W["ml_b_f"][L], 4)
            ng_bc = bcast_load("ml_ng", W["ml_norm_g"][L], 512)
            cw = kb.sbuf("ml_cw", [128, 8, 4], F32)
            cb = kb.sbuf("ml_cb", [128, 8], F32)
            for w_ in range(4):
                kb.dma("sp", cw[:, :, w_], W["ml_conv_w"][L][w_].rearrange("(c p) -> p c", p=128), ds_c, writes=[cw], allow_slow_non_contiguous=True)
            kb.dma("sp", cb[:, :], W["ml_conv_b"][L].rearrange("(c p) -> p c", p=128), ds_c, writes=[cb], allow_slow_non_contiguous=True)
            wif, _ = load_w(w_in[:, COL["c_if"]:COL["c_if"] + 8], 8)
            G = kb.sbuf("ml_G", [128, 16, 8], F32)
            ps = PS[0]
            def fng(pe):
                ins = None
                for i in range(NT):
                    for kc in range(8):
                        ins = pe.matmul(ps[:, i * 8:(i + 1) * 8], XT[i // 4][:, kc, (i % 4) * 128:(i % 4 + 1) * 128], wif[:, kc, 0:8], start=(kc == 0), stop=(kc == 7))
                return ins
            kb.op("pe", fng, reads=XT + [wif], writes=[ps])
            kb.op("dve", lambda v: v.tensor_copy(G[:, :, :], ps[:, 0:128].rearrange("p (c e) -> p c e", e=8)), reads=[ps], writes=[G])
            def t64(name):
                return kb.sbuf(name, [128, 16, 4], F32)
            li, lf, a_t, Rb, u_t, Rum, eend, mloc, adec, bbt, Mt, gsc, eint, thr, tmpg = [t64(n_) for n_ in
                ("li", "lf", "a_t", "Rb", "u_t", "Rum", "eend", "mloc", "adec", "bbt", "Mt", "gsc", "eint", "thr", "tmpg")]
            mprev = kb.sbuf("mprev", [128, 17, 4], F32)
            def f2(t_):
                return t_[:, :, :].rearrange("p c h -> p (c h)")
            bcb = lambda t_: t_[:, 0:4].unsqueeze(1).to_broadcast([128, 16, 4])
            kb.op("dve", lambda v: v.tensor_tensor(li[:, :, :], G[:, :, 0:4], bcb(bi_bc), op=ALU.add), reads=[G, bi_bc], writes=[li])
            kb.op("dve", lambda v: v.tensor_tensor(lf[:, :, :], G[:, :, 4:8], bcb(bf_bc), op=ALU.add), reads=[G, bf_bc], writes=[lf])
            kb.op("act", lambda a: a.activation(out=lf[:, :, :], in_=lf[:, :, :], func=AF.Exp, scale=-1.0), reads=[lf], writes=[lf])
            kb.op("dve", lambda v: v.tensor_scalar_add(lf[:, :, :], lf[:, :, :], 1.0), reads=[lf], writes=[lf])
            kb.op("act", lambda a: a.activation(out=lf[:, :, :], in_=lf[:, :, :], func=AF.Ln), reads=[lf], writes=[lf])
            kb.op("dve", lambda v: v.tensor_scalar_mul(lf[:, :, :], lf[:, :, :], -1.0), reads=[lf], writes=[lf])
            ps = PS[1]
            mm(ps, ps[:, 0:64], [(trif[:, :], f2(lf))], reads=[trif, lf])
            kb.op("dve", lambda v: v.tensor_copy(f2(a_t), ps[:, 0:64]), reads=[ps], writes=[a_t])
            mm(ps, ps[:, 64:128], [(onesf[:, :], f2(lf))], reads=[onesf, lf])
            kb.op("dve", lambda v: v.tensor_copy(f2(Rb), ps[:, 64:128]), reads=[ps], writes=[Rb])
            kb.op("dve", lambda v: v.tensor_tensor(f2(u_t), f2(li), f2(a_t), op=ALU.subtract), reads=[li, a_t], writes=[u_t])
            ps2 = PS[2]
            kb.op("pe", lambda pe: pe.transpose(ps2[0:64, 0:128], f2(u_t), ident_f[:, :]), reads=[u_t, ident_f], writes=[ps2])
            um = kb.sbuf("ml_um", [64, 1], F32)
            dg = kb.sbuf("ml_dg", [64, 64], F32)
            kb.op("dve", lambda v: v.tensor_reduce(um[:, :], ps2[0:64, 0:128], axis=AX.X, op=ALU.max), reads=[ps2], writes=[um])
            kb.op("dve", lambda v: v.tensor_scalar_mul(dg[:, :], ident_f[0:64, 0:64], um[:, 0:1]), reads=[ident_f, um], writes=[dg])
            mm(ps2, ps2[:, 128:192], [(onesf[0:64, :], dg[:, :])], reads=[onesf, dg])
            kb.op("dve", lambda v: v.tensor_copy(f2(Rum), ps2[:, 128:192]), reads=[ps2], writes=[Rum])
            kb.op("dve", lambda v: v.tensor_tensor(f2(eend), f2(u_t), f2(Rum), op=ALU.subtract), reads=[u_t, Rum], writes=[eend])
            kb.op("act", lambda a: a.activation(out=f2(eend), in_=f2(eend), func=AF.Exp), reads=[eend], writes=[eend])
            kb.op("dve", lambda v: v.tensor_tensor(f2(mloc), f2(Rb), f2(Rum), op=ALU.add), reads=[Rb, Rum], writes=[mloc])
            kb.op("dve", lambda v: v.memset(mprev[:, 0, :], 0.0), writes=[mprev])
            for c in range(16):
                kb.op("dve", lambda v, c=c: v.tensor_tensor(tmpg[:, c, :], Rb[:, c, :], mprev[:, c, :], op=ALU.add), reads=[Rb, mprev], writes=[tmpg])
                kb.op("dve", lambda v, c=c: v.tensor_tensor(mprev[:, c + 1, :], tmpg[:, c, :], mloc[:, c, :], op=ALU.max), reads=[tmpg, mloc], writes=[mprev])
            kb.op("dve", lambda v: v.tensor_tensor(adec[:, :, :], tmpg[:, :, :], mprev[:, 1:17, :], op=ALU.subtract), reads=[tmpg, mprev], writes=[adec])
            kb.op("act", lambda a: a.activation(out=f2(adec), in_=f2(adec), func=AF.Exp), reads=[adec], writes=[adec])
            kb.op("dve", lambda v: v.tensor_tensor(bbt[:, :, :], mloc[:, :, :], mprev[:, 1:17, :], op=ALU.subtract), reads=[mloc, mprev], writes=[bbt])
            kb.op("act", lambda a: a.activation(out=f2(bbt), in_=f2(bbt), func=AF.Exp), reads=[bbt], writes=[bbt])
            kb.op("dve", lambda v: v.tensor_tensor(Mt[:, :, :], mprev[:, 0:16, :], Rum[:, :, :], op=ALU.max), reads=[mprev, Rum], writes=[Mt])
            kb.op("dve", lambda v: v.tensor_tensor(f2(gsc), f2(Rum), f2(Mt), op=ALU.subtract), reads=[Rum, Mt], writes=[gsc])
            kb.op("act", lambda a: a.activation(out=f2(gsc), in_=f2(gsc), func=AF.Exp), reads=[gsc], writes=[gsc])
            kb.op("dve", lambda v: v.tensor_tensor(eint[:, :, :], mprev[:, 0:16, :], Mt[:, :, :], op=ALU.subtract), reads=[mprev, Mt], writes=[eint])
            kb.op("act", lambda a: a.activation(out=f2(eint), in_=f2(eint), func=AF.Exp), reads=[eint], writes=[eint])
            kb.op("dve", lambda v: v.tensor_scalar_mul(f2(eint), f2(eint), 128.0 ** -0.5), reads=[eint], writes=[eint])
            kb.op("dve", lambda v: v.tensor_tensor(f2(thr), f2(a_t), f2(Mt), op=ALU.add), reads=[a_t, Mt], writes=[thr])
            kb.op("act", lambda a: a.activation(out=f2(thr), in_=f2(thr), func=AF.Exp, scale=-1.0), reads=[thr], writes=[thr])

            pre = [kb.sbuf(f"ml_pre{i}", [128, 3 + T], F32) for i in range(2)]
            cacc = [kb.sbuf(f"ml_cacc{i}", [128, T], F32) for i in range(2)]
            for b_ in range(2):
                kb.op("dve", lambda v, b_=b_: v.memset(pre[b_][:, 0:3], 0.0), writes=[pre[b_]])
            QT = [kb.sbuf(f"ml_qT{i}", [128, T], BF16) for i in range(2)]
            KT = [kb.sbuf(f"ml_kT{i}", [128, T], BF16) for i in range(2)]
            KTOK = [kb.sbuf(f"ml_ktok{i}", [128, 16, 128], BF16) for i in range(2)]
            VE = [kb.sbuf(f"ml_ve{i}", [128, 16, 129], BF16) for i in range(2)]
            SOZ = [kb.sbuf(f"ml_soz{i}", [128, 16, 128], F32) for i in range(2)]
            so_ = [kb.sbuf(f"ml_so{i}", [128, 128], F32) for i in range(2)]
            sz_ = [kb.sbuf(f"ml_sz{i}", [128, 128], F32) for i in range(2)]
            Cst = kb.sbuf("ml_Cst", [128, 129], F32)
            Cbf = kb.sbuf("ml_Cbf", [128, 129], BF16)
            ctmp = kb.sbuf("ml_ctmp", [128, 129], F32)
            ATs = [kb.sbuf(f"ml_AT{i}", [128, 128], BF16) for i in range(2)]
            nds = [kb.sbuf(f"ml_nd{i}", [128, 129], F32) for i in range(2)]
            d1s = [kb.sbuf(f"ml_d1{i}", [128, 2], F32) for i in range(2)]
            hhs = [kb.sbuf(f"ml_hh{i}", [128, 128], F32) for i in range(2)]
            sts = [kb.sbuf(f"ml_st{i}", [128, 8], F32) for i in range(2)]
            yts = [kb.sbuf(f"ml_yt{i}", [128, 128], BF16) for i in range(2)]
            pn = 0
            for h in range(4):
                hb = h % 2
                ws, wd = wslot()
                for j, nm in enumerate(("c_q", "c_k", "c_v", "c_o", "c_z")):
                    load_w(w_in[:, COL[nm] + h * 128:COL[nm] + (h + 1) * 128], 128, col0=j * 128, slot=(ws, wd))
                for which, dstT in ((0, QT[hb]), (1, KT[hb])):
                    pr, ca = pre[which], cacc[which]
                    ch = which * 4 + h
                    for tb in range(NTB):
                        ps = PS[pn % 2]
                        pn += 1
                        mm(ps, ps[:, :], [(ws[:, kc, which * 128:(which + 1) * 128], XT[tb][:, kc, :]) for kc in range(8)], reads=[ws, XT[tb]])
                        kb.op("act", lambda a, pr=pr, ps=ps, tb=tb: a.copy(out=pr[:, 3 + tb * 512:3 + (tb + 1) * 512], in_=ps[:, :]), reads=[ps], writes=[pr])
                    kb.op("dve", lambda v, ca=ca, pr=pr, ch=ch: v.tensor_scalar(ca[:, :], pr[:, 3:3 + T], cw[:, ch, 3:4], cb[:, ch:ch + 1], op0=ALU.mult, op1=ALU.add), reads=[pr, cw, cb], writes=[ca])
                    for w_ in (2, 1, 0):
                        kb.op("dve", lambda v, ca=ca, pr=pr, ch=ch, w_=w_: v.scalar_tensor_tensor(ca[:, :], pr[:, w_:w_ + T], cw[:, ch, w_:w_ + 1], ca[:, :], op0=ALU.mult, op1=ALU.add), reads=[pr, cw, ca], writes=[ca])
                    kb.op("act", lambda a, dstT=dstT, ca=ca: a.activation(out=dstT[:, :], in_=ca[:, :], func=AF.Silu), reads=[ca], writes=[dstT])
                for half in range(2):
                    ps = PS[6 + half]
                    pb = ps[:, :].bitcast(BF16)
                    def fnk(pe, half=half, pb=pb):
                        ins = None
                        for cc in range(8):
                            c = half * 8 + cc
                            ins = pe.transpose(pb[:, cc * 128:(cc + 1) * 128], KT[hb][:, c * 128:(c + 1) * 128], ident_b[:, :])
                        return ins
                    kb.op("pe", fnk, reads=[KT[hb], ident_b], writes=[ps])
                    kb.op("dve", lambda v, half=half, pb=pb: v.tensor_copy(KTOK[hb][:, half * 8:(half + 1) * 8, :], pb.rearrange("p (c d) -> p c d", c=8)), reads=[ps], writes=[KTOK[hb]])
                for i in range(NT):
                    ps = PS[pn % 2]
                    pn += 1
                    tsl = slice((i % 4) * 128, (i % 4 + 1) * 128)
                    mm(ps, ps[:, 0:384], [(XT[i // 4][:, kc, tsl], ws[:, kc, 256:640]) for kc in range(8)], reads=[ws, XT[i // 4]])
                    idx = i * 4 + h
                    kb.op("dve", lambda v, i=i, ps=ps, idx=idx: v.tensor_scalar_mul(VE[hb][:, i, 0:128], ps[:, 0:128], f2(eend)[:, idx:idx + 1]), reads=[ps, eend], writes=[VE[hb]])
                    kb.op("dve", lambda v, i=i, idx=idx: v.tensor_copy(VE[hb][:, i, 128:129], f2(eend)[:, idx:idx + 1]), reads=[eend], writes=[VE[hb]])
                    s1, s2 = so_[i % 2], sz_[i % 2]
                    kb.op("act", lambda a, s1=s1, ps=ps: a.activation(out=s1[:, :], in_=ps[:, 128:256], func=AF.Sigmoid), reads=[ps], writes=[s1])
                    kb.op("act", lambda a, s2=s2, ps=ps: a.activation(out=s2[:, :], in_=ps[:, 256:384], func=AF.Silu), reads=[ps], writes=[s2])
                    kb.op("dve", lambda v, s1=s1, s2=s2: v.tensor_tensor(s1[:, :], s1[:, :], s2[:, :], op=ALU.mult), reads=[s1, s2], writes=[s1])
                    kb.op("dve", lambda v, s1=s1, i=i: v.tensor_tensor(SOZ[hb][:, i, :], s1[:, :], ng_bc[:, h * 128:(h + 1) * 128], op=ALU.mult), reads=[s1, ng_bc], writes=[SOZ[hb]])
                kb.op("dve", lambda v: v.memset(Cst[:, :], 0.0), writes=[Cst])
                kb.op("dve", lambda v: v.memset(Cbf[:, :], 0.0), writes=[Cbf])
                for c in range(16):
                    cs = slice(c * 128, (c + 1) * 128)
                    idx = c * 4 + h
                    pa, p1, p2, pc, pt_ = PS[2], PS[3], PS[4], PS[5], PS[6 + c % 2]
                    AT, nd, d1, hh, st_, yt_ = ATs[c % 2], nds[c % 2], d1s[c % 2], hhs[c % 2], sts[c % 2], yts[c % 2]
                    mm(pa, pa[:, 0:128], [(KT[hb][:, cs], QT[hb][:, cs])], reads=[KT[hb], QT[hb]])
                    kb.op("dve", lambda v, AT=AT, pa=pa: v.tensor_tensor(AT[:, :], pa[:, 0:128], maskS[:, :], op=ALU.mult), reads=[pa, maskS], writes=[AT])
                    mm(p1, p1[:, 0:129], [(AT[:, :], VE[hb][:, c, :])], reads=[AT, VE[hb]])
                    mm(p2, p2[:, 0:129], [(QT[hb][:, cs], Cbf[:, :])], reads=[QT[hb], Cbf])
                    mm(pc, pc[:, 0:129], [(KTOK[hb][:, c, :], VE[hb][:, c, :])], reads=[KTOK[hb], VE[hb]])
                    kb.op("dve", lambda v, nd=nd, p1=p1, idx=idx: v.tensor_scalar_mul(nd[:, :], p1[:, 0:129], f2(gsc)[:, idx:idx + 1]), reads=[p1, gsc], writes=[nd])
                    kb.op("dve", lambda v, nd=nd, p2=p2, idx=idx: v.scalar_tensor_tensor(nd[:, :], p2[:, 0:129], f2(eint)[:, idx:idx + 1], nd[:, :], op0=ALU.mult, op1=ALU.add), reads=[p2, eint, nd], writes=[nd])
                    kb.op("dve", lambda v, nd=nd, d1=d1: v.tensor_scalar_mul(d1[:, 0:1], nd[:, 128:129], -1.0), reads=[nd], writes=[d1])
                    kb.op("dve", lambda v, nd=nd, d1=d1: v.tensor_tensor(d1[:, 0:1], d1[:, 0:1], nd[:, 128:129], op=ALU.max), reads=[nd, d1], writes=[d1])
                    kb.op("dve", lambda v, d1=d1, idx=idx: v.tensor_tensor(d1[:, 0:1], d1[:, 0:1], f2(thr)[:, idx:idx + 1], op=ALU.max), reads=[thr, d1], writes=[d1])
                    kb.op("dve", lambda v, d1=d1: v.reciprocal(d1[:, 1:2], d1[:, 0:1]), reads=[d1], writes=[d1])
                    kb.op("dve", lambda v, hh=hh, nd=nd, d1=d1: v.tensor_scalar_mul(hh[:, :], nd[:, 0:128], d1[:, 1:2]), reads=[nd, d1], writes=[hh])
                    kb.op("dve", lambda v, st_=st_, hh=hh: v.bn_stats(st_[:, 0:6], hh[:, :]), reads=[hh], writes=[st_])
                    kb.op("dve", lambda v, st_=st_: v.bn_aggr(st_[:, 6:8], st_[:, 0:6]), reads=[st_], writes=[st_])
                    rstd_(st_, 7)
                    kb.op("dve", lambda v, hh=hh, st_=st_: v.tensor_scalar(hh[:, :], hh[:, :], st_[:, 6:7], st_[:, 7:8], op0=ALU.subtract, op1=ALU.mult), reads=[hh, st_], writes=[hh])
                    kb.op("dve", lambda v, yt_=yt_, hh=hh, c=c: v.tensor_tensor(yt_[:, :], hh[:, :], SOZ[hb][:, c, :], op=ALU.mult), reads=[hh, SOZ[hb]], writes=[yt_])
                    ptb = pt_[:, :].bitcast(BF16)
                    kb.op("pe", lambda pe, ptb=ptb, yt_=yt_: pe.transpose(ptb[:, 0:128], yt_[:, :], ident_b[:, :]), reads=[yt_, ident_b], writes=[pt_])
                    ytile = YT["c"][c // 4]
                    kb.op("act", lambda a, ytile=ytile, ptb=ptb, c=c: a.copy(out=ytile[:, h, (c % 4) * 128:(c % 4 + 1) * 128], in_=ptb[:, 0:128]), reads=[pt_], writes=[ytile])
                    kb.op("dve", lambda v, pc=pc, idx=idx: v.tensor_scalar_mul(ctmp[:, :], pc[:, 0:129], f2(bbt)[:, idx:idx + 1]), reads=[pc, bbt], writes=[ctmp])
                    kb.op("dve", lambda v, idx=idx: v.scalar_tensor_tensor(Cst[:, :], Cst[:, :], f2(adec)[:, idx:idx + 1], ctmp[:, :], op0=ALU.mult, op1=ALU.add), reads=[Cst, adec, ctmp], writes=[Cst])
                    kb.op("act", lambda a: a.copy(out=Cbf[:, :], in_=Cst[:, :]), reads=[Cst], writes=[Cbf])
            if L == 0:
                for tb in range(NTB):
                    dbg_out(f"ycT{tb}", YT["c"][tb], YT["c"][tb][:, :, :], [128, 4, 512], BF16)
            kb.pop()

        if "D" in stages:
            kb.push()
            MT = [kb.sbuf(f"mT{i}", [128, 8, 512], BF16) for i in range(NTB)]
            kb.push()
            alloc_ws()
            brs = [br for br in "abc" if br.upper() in stages]
            gs = [kb.sbuf(f"gsig{i}", [128, 512], F32) for i in range(2)]
            acc = [kb.sbuf(f"macc{i}", [128, 512], F32) for i in range(2)]
            n = 0
            for fc in range(8):
                ws, wd = wslot()
                for bi, br in enumerate(brs):
                    k = "abc".index(br)
                    load_w(W["w_br_" + br][L][:, fc * 128:(fc + 1) * 128], 128, nk=4, col0=bi * 128, slot=(ws, wd))
                    load_w(w_in[:, COL["m_g"] + k * D + fc * 128: COL["m_g"] + k * D + (fc + 1) * 128], 128, nk=8, col0=384 + bi * 128 - 0, slot=(ws, wd)) if False else None
                ws2, wd2 = wslot()
                for bi, br in enumerate(brs):
                    k = "abc".index(br)
                    load_w(w_in[:, COL["m_g"] + k * D + fc * 128: COL["m_g"] + k * D + (fc + 1) * 128], 128, nk=8, col0=bi * 128, slot=(ws2, wd2))
                for tb in range(NTB):
                    a_ = acc[n % 2]
                    n += 1
                    for bi, br in enumerate(brs):
                        pp, pg = PS[(2 * bi) % 6], PS[(2 * bi + 1) % 6]
                        yt = YT[br][tb]
                        xt = XT[tb]
                        mm(pp, pp[:, :], [(ws[:, kc, bi * 128:(bi + 1) * 128], yt[:, kc, :]) for kc in range(4)], reads=[ws, yt])
                        mm(pg, pg[:, :], [(ws2[:, kc, bi * 128:(bi + 1) * 128], xt[:, kc, :]) for kc in range(8)], reads=[ws2, xt])
                        g_ = gs[bi % 2]
                        kb.op("act", lambda a, g_=g_, pg=pg: a.activation(out=g_[:, :], in_=pg[:, :], func=AF.Sigmoid), reads=[pg], writes=[g_])
                        last = (bi == len(brs) - 1)
                        dst_t = MT[tb] if last else a_
                        dst_ap = MT[tb][:, fc, :] if last else a_[:, :]
                        if bi == 0:
                            kb.op("dve", lambda v, dst_ap=dst_ap, g_=g_, pp=pp: v.tensor_tensor(dst_ap, g_[:, :], pp[:, :], op=ALU.mult), reads=[g_, pp], writes=[dst_t])
                        else:
                            kb.op("dve", lambda v, g_=g_, pp=pp: v.tensor_tensor(g_[:, :], g_[:, :], pp[:, :], op=ALU.mult), reads=[g_, pp], writes=[g_])
                            kb.op("dve", lambda v, dst_ap=dst_ap, g_=g_, a_=a_: v.tensor_tensor(dst_ap, g_[:, :], a_[:, :], op=ALU.add), reads=[g_, a_], writes=[dst_t])
            if L == 0:
                for tb in range(NTB):
                    dbg_out(f"mT{tb}", MT[tb], MT[tb][:, :, :], [128, 8, 512], BF16)
            kb.pop()
            kb.push()

            wo = kb.sbuf("w_out", [128, 8, D], BF16)
            wgt = kb.sbuf("w_pg", [128, 8, D], BF16)
            wpl = kb.sbuf("w_pl", [128, 2, D], BF16)
            dwo = kb.dsem()
            for hcol in range(2):
                kb.dma("pool", wo[:, :, hcol * 512:(hcol + 1) * 512], W["w_out"][L][:, hcol * 512:(hcol + 1) * 512].rearrange("(kc p) n -> p kc n", p=128), dwo, writes=[wo], part=True)
                kb.dma("pool", wgt[:, :, hcol * 512:(hcol + 1) * 512], W["ple_gate"][L][:, hcol * 512:(hcol + 1) * 512].rearrange("(kc p) n -> p kc n", p=128), dwo, writes=[wgt], part=True)
            kb.dma("pool", wpl[:, :, :], W["ple_w"][L].rearrange("(kc p) n -> p kc n", p=128), dwo, writes=[wpl])
            lng = bcast_load("ln_g", W["ln_g"][L], D)
            lnb = bcast_load("ln_b", W["ln_b"][L], D)
            xr = [kb.sbuf(f"xres{i}", [128, D], F32) for i in range(2)]
            xrd = [kb.dsem() for _ in range(2)]
            pt = [kb.sbuf(f"ptok{i}", [128, 256], BF16) for i in range(2)]
            ptd = [kb.dsem() for _ in range(2)]
            rr = [kb.sbuf(f"r{i}", [128, D], F32) for i in range(2)]
            rb = [kb.sbuf(f"rb{i}", [128, D], BF16) for i in range(2)]
            rT = [kb.sbuf(f"rT{i}", [128, 8, 128], BF16) for i in range(2)]
            pT = [kb.sbuf(f"pT{i}", [128, 2, 128], BF16) for i in range(2)]
            sg_ = [kb.sbuf(f"sgate{i}", [128, D], F32) for i in range(2)]
            st = [kb.sbuf(f"lnst{i}", [128, 16], F32) for i in range(2)]
            xo = rr
            xob = [kb.sbuf(f"xob{i}", [128, D], BF16) for i in range(2)]
            xod = [kb.dsem() for _ in range(2)]
            for i in range(NT):
                b = i % 2
                mt_ = MT[i // 4]
                tsl = slice((i % 4) * 128, (i % 4 + 1) * 128)
                kb.dma("sp", xr[b][:, :], res_src.h[i * 128:(i + 1) * 128, :], xrd[b], reads=[res_src], writes=[xr[b]])
                kb.dma("pool", pt[b][:, :], p_d[L, i * 128:(i + 1) * 128, :], ptd[b], writes=[pt[b]])
                for hcol in range(2):
                    ps = PS[hcol]
                    cs = slice(hcol * 512, (hcol + 1) * 512)
                    mm(ps, ps[:, :], [(mt_[:, kc, tsl], wo[:, kc, cs]) for kc in range(8)], reads=[mt_, wo])
                    kb.op("dve", lambda v, b=b, ps=ps, cs=cs: v.scalar_tensor_tensor(rr[b][:, cs], xr[b][:, cs], ALPHA, ps[:, :], op0=ALU.mult, op1=ALU.add), reads=[xr[b], ps], writes=[rr[b]])
                kb.op("act", lambda a, b=b: a.copy(out=rb[b][:, :], in_=rr[b][:, :]), reads=[rr[b]], writes=[rb[b]])
                ps = PS[2]
                pb = ps[:, :].bitcast(BF16)
                def fn(pe, b=b, pb=pb):
                    ins = None
                    for kc in range(8):
                        ins = pe.transpose(pb[:, kc * 128:(kc + 1) * 128], rb[b][:, kc * 128:(kc + 1) * 128], ident_b[:, :])
                    return ins
                kb.op("pe", fn, reads=[rb[b], ident_b], writes=[ps])
                kb.op("dve", lambda v, b=b, pb=pb: v.tensor_copy(rT[b][:, :, :], pb.rearrange("p (k t) -> p k t", k=8)), reads=[ps], writes=[rT[b]])
                ps = PS[3]
                pb2 = ps[:, :].bitcast(BF16)
                def fn2(pe, b=b, pb2=pb2):
                    ins = None
                    for kc in range(2):
                        ins = pe.transpose(pb2[:, kc * 128:(kc + 1) * 128], pt[b][:, kc * 128:(kc + 1) * 128], ident_b[:, :])
                    return ins
                kb.op("pe", fn2, reads=[pt[b], ident_b], writes=[ps])
                kb.op("dve", lambda v, b=b, pb2=pb2: v.tensor_copy(pT[b][:, :, :], pb2[:, 0:256].rearrange("p (k t) -> p k t", k=2)), reads=[ps], writes=[pT[b]])
                for hcol in range(2):
                    cs = slice(hcol * 512, (hcol + 1) * 512)
                    pg, pp = PS[4 + hcol], PS[6 + hcol]
                    mm(pg, pg[:, :], [(rT[b][:, kc, :], wgt[:, kc, cs]) for kc in range(8)], reads=[rT[b], wgt])
                    mm(pp, pp[:, :], [(pT[b][:, kc, :], wpl[:, kc, cs]) for kc in range(2)], reads=[pT[b], wpl])
                    kb.op("act", lambda a, b=b, pg=pg, cs=cs: a.activation(out=sg_[b][:, cs], in_=pg[:, :], func=AF.Sigmoid), reads=[pg], writes=[sg_[b]])
                    kb.op("dve", lambda v, b=b, pp=pp, cs=cs: v.tensor_tensor(sg_[b][:, cs], sg_[b][:, cs], pp[:, :], op=ALU.mult), reads=[sg_[b], pp], writes=[sg_[b]])
                kb.op("dve", lambda v, b=b: v.tensor_tensor(rr[b][:, :], rr[b][:, :], sg_[b][:, :], op=ALU.add), reads=[rr[b], sg_[b]], writes=[rr[b]])
                s_ = st[b]
                kb.op("dve", lambda v, b=b, s_=s_: v.bn_stats(s_[:, 0:6], rr[b][:, 0:512]), reads=[rr[b]], writes=[s_])
                kb.op("dve", lambda v, b=b, s_=s_: v.bn_stats(s_[:, 6:12], rr[b][:, 512:1024]), reads=[rr[b], s_], writes=[s_])
                kb.op("dve", lambda v, s_=s_: v.bn_aggr(s_[:, 12:14], s_[:, 0:12]), reads=[s_], writes=[s_])
                rstd_(s_, 13)
                kb.op("dve", lambda v, b=b, s_=s_: v.tensor_scalar(rr[b][:, :], rr[b][:, :], s_[:, 12:13], s_[:, 13:14], op0=ALU.subtract, op1=ALU.mult), reads=[rr[b], s_], writes=[rr[b]])
                kb.op("dve", lambda v, b=b: v.tensor_tensor(rr[b][:, :], rr[b][:, :], lng[:, :], op=ALU.mult), reads=[rr[b], lng], writes=[rr[b]])
                kb.op("dve", lambda v, b=b: v.tensor_tensor(rr[b][:, :], rr[b][:, :], lnb[:, :], op=ALU.add), reads=[rr[b], lnb], writes=[rr[b]])
                kb.dma("sp", dst_d.h[i * 128:(i + 1) * 128, :], xo[b][:, :], xod[b], reads=[xo[b]], writes=[dst_d])
                if L + 1 < nlayers:
                    kb.op("act", lambda a, b=b: a.copy(out=xob[b][:, :], in_=xo[b][:, :]), reads=[xo[b]], writes=[xob[b]])
                    tok_to_XT(xob[b], xob[b], i, 2)
            kb.pop()
            kb.pop()
        kb.pop()

    kb.finish([out_d] + list(dbg_d.values()))
    kb.es.close()
    return nc


_NC_CACHE = {}


def _run(inputs, stages, dbg, cores, nlayers=DEPTH):
    key = (tuple(stages), tuple(dbg), nlayers)
    if key not in _NC_CACHE:
        _NC_CACHE[key] = build(stages, dbg, nlayers)
    nc = _NC_CACHE[key]
    in_maps = []
    for b in cores:
        m = {"x": np.ascontiguousarray(inputs["x"][b]), "p": np.ascontiguousarray(inputs["p"][:, b])}
        for k in ("w_in", "cmp_pos_k", "cmp_pos_v", "cmp_wk1", "cmp_wk2", "cmp_wv1", "cmp_wv2", "sg_ln_g", "sg_ln_b",
                  "sg_w", "sg_b", "ml_conv_w", "ml_conv_b", "ml_b_i", "ml_b_f", "ml_norm_g", "w_br_a", "w_br_b",
                  "w_br_c", "w_out", "ple_w", "ple_gate", "ln_g", "ln_b"):
            m[k] = np.ascontiguousarray(inputs[k], dtype=np.float32)
        for k, v in CONSTS.items():
            m["c_" + k] = v
        in_maps.append(m)
    res = run_bass_kernel_spmd(nc, in_maps, core_ids=list(range(len(cores))))
    return res


def kernel(**inputs):
    inputs = {k: np.asarray(v) for k, v in inputs.items()}
    res = _run(inputs, ("A", "B", "C", "D"), (), list(range(8)))
    return np.stack([r["out"] for r in res.results], axis=0).astype(np.float32)
```
